# Optimizing a Trainium2 kernel written in Bass

```python
import math
import jax, jax.numpy as jnp
from jax import lax
import numpy as np


D_MODEL = 1024
BATCH = 2
SEQ = 16384
DEPTH = 4

CHUNK = 64
N_MIXERS = 2
N_HEADS = 16
HEAD_DIM = D_MODEL // N_HEADS
IDX_HEADS = 8
IDX_DIM = 64
TOPK_MAX = 256
Q_BLOCK = 128
ROPE_THETA = 10000.0
SSM_GROUP = 16
N_GROUPS = D_MODEL // SSM_GROUP
SSM_STATE = 64
D_FF = 2816
CONV_W = 3
EPS = 1e-6
NEG = -1e30
N_ATTN = (DEPTH + 1) // 2
N_SSM = DEPTH // 2
IN_COLS = 3 * D_MODEL + IDX_HEADS * IDX_DIM + IDX_DIM + IDX_HEADS

kernel_name = "chunk_causal_dsa_s5_hybrid"


def rms_norm(x, g):
    xf = x.astype(jnp.float32)
    y = xf * lax.rsqrt(jnp.mean(xf * xf, axis=-1, keepdims=True) + EPS)
    return (y * g.astype(jnp.float32)).astype(x.dtype)


def rope(x, positions):
    half = x.shape[-1] // 2
    inv = ROPE_THETA ** (-jnp.arange(half, dtype=jnp.float32) / half)
    ang = positions.astype(jnp.float32)[..., None] * inv
    cos = jnp.cos(ang)[:, :, None, :]
    sin = jnp.sin(ang)[:, :, None, :]
    xf = x.astype(jnp.float32)
    x1, x2 = xf[..., :half], xf[..., half:]
    return jnp.concatenate([x1 * cos - x2 * sin, x2 * cos + x1 * sin], axis=-1).astype(x.dtype)


def dsa_attention(h, positions, w_in, q_gain, k_gain, w_out):
    B, L, D = h.shape
    proj = h @ w_in
    s1 = D
    s2 = 2 * D
    s3 = 3 * D
    s4 = s3 + IDX_HEADS * IDX_DIM
    s5 = s4 + IDX_DIM
    q = proj[..., :s1].reshape(B, L, N_HEADS, HEAD_DIM)
    k = proj[..., s1:s2].reshape(B, L, N_HEADS, HEAD_DIM)
    v = proj[..., s2:s3].reshape(B, L, N_HEADS, HEAD_DIM)
    qi = proj[..., s3:s4].reshape(B, L, IDX_HEADS, IDX_DIM)
    ki = proj[..., s4:s5].reshape(B, L, 1, IDX_DIM)
    wi = proj[..., s5:].astype(jnp.float32) * IDX_HEADS ** -0.5
    q = rope(rms_norm(q, q_gain), positions)
    k = rope(rms_norm(k, k_gain), positions)
    qi = rope(qi, positions).astype(jnp.float32)
    ki = rope(ki, positions)[:, :, 0, :].astype(jnp.float32)
    topk = min(TOPK_MAX, L // 4)
    key_chunk = jnp.arange(L) // CHUNK
    n_blk = L // Q_BLOCK

    def block(i):
        start = i * Q_BLOCK
        qb = lax.dynamic_slice_in_dim(q, start, Q_BLOCK, axis=1)
        qib = lax.dynamic_slice_in_dim(qi, start, Q_BLOCK, axis=1)
        wib = lax.dynamic_slice_in_dim(wi, start, Q_BLOCK, axis=1)
        q_chunk = (start + jnp.arange(Q_BLOCK)) // CHUNK
        logits = jnp.einsum('bqhd,bsd->bqhs', qib, ki) * IDX_DIM ** -0.5
        score = jnp.einsum('bqh,bqhs->bqs', wib, jax.nn.relu(logits))
        admissible = key_chunk[None, :] <= q_chunk[:, None]
        score = jnp.where(admissible[None], score, NEG)
        _, idx = lax.top_k(score, topk)
        valid = (idx // CHUNK) <= q_chunk[None, :, None]
        kg = jax.vmap(lambda kb, ib: kb[ib])(k, idx)
        vg = jax.vmap(lambda vb, ib: vb[ib])(v, idx)
        s = jnp.einsum('bqhd,bqkhd->bqhk', qb.astype(jnp.float32), kg.astype(jnp.float32)) * HEAD_DIM ** -0.5
        s = jnp.where(valid[:, :, None, :], s, NEG)
        p = jax.nn.softmax(s, axis=-1)
        o = jnp.einsum('bqhk,bqkhd->bqhd', p.astype(v.dtype), vg)
        return o.reshape(B, Q_BLOCK, D)

    out = lax.map(block, jnp.arange(n_blk))
    out = out.transpose(1, 0, 2, 3).reshape(B, L, D)
    return out @ w_out


def _complex_scan_op(e1, e2):
    a1r, a1i, b1r, b1i = e1
    a2r, a2i, b2r, b2i = e2
    ar = a2r * a1r - a2i * a1i
    ai = a2r * a1i + a2i * a1r
    br = a2r * b1r - a2i * b1i + b2r
    bi = a2r * b1i + a2i * b1r + b2i
    return (ar, ai, br, bi)


def s5_mixer(h, a_re, a_im, log_dt, b_re, b_im, c_re, c_im, d_skip, w_glu):
    B, L, D = h.shape
    u = h.astype(jnp.float32).reshape(B, L, N_GROUPS, SSM_GROUP)
    a_re = a_re.astype(jnp.float32)
    a_im = a_im.astype(jnp.float32)
    dt = jnp.exp(log_dt.astype(jnp.float32))[:, None]
    decay = jnp.exp(a_re * dt)
    ab_re = decay * jnp.cos(a_im * dt)
    ab_im = decay * jnp.sin(a_im * dt)
    den = a_re * a_re + a_im * a_im
    nr = ab_re - 1.0
    ni = ab_im
    coef_re = ((nr * a_re + ni * a_im) / den)[..., None]
    coef_im = ((ni * a_re - nr * a_im) / den)[..., None]
    br = b_re.astype(jnp.float32)
    bim = b_im.astype(jnp.float32)
    bb_re = coef_re * br - coef_im * bim
    bb_im = coef_re * bim + coef_im * br
    bu_re = jnp.einsum('blgc,gpc->blgp', u, bb_re)
    bu_im = jnp.einsum('blgc,gpc->blgp', u, bb_im)
    ar = jnp.broadcast_to(ab_re, bu_re.shape)
    ai = jnp.broadcast_to(ab_im, bu_re.shape)
    _, _, x_re, x_im = lax.associative_scan(_complex_scan_op, (ar, ai, bu_re, bu_im), axis=1)
    y = (jnp.einsum('blgp,gcp->blgc', x_re, c_re.astype(jnp.float32))
         - jnp.einsum('blgp,gcp->blgc', x_im, c_im.astype(jnp.float32)))
    y = y.reshape(B, L, D) + d_skip.astype(jnp.float32) * u.reshape(B, L, D)
    g = jax.nn.gelu(y).astype(h.dtype)
    ga, gb = jnp.split(g @ w_glu, 2, axis=-1)
    return ga * jax.nn.sigmoid(gb)


def conv_ffn(h, w_up, conv_w, conv_b, w_down):
    up = h @ w_up
    up = lax.conv_general_dilated(
        up, conv_w[:, None, :], window_strides=(1,), padding=[(CONV_W - 1, 0)],
        dimension_numbers=('NWC', 'WIO', 'NWC'), feature_group_count=2 * D_FF) + conv_b
    val, gate = jnp.split(up, 2, axis=-1)
    return (jax.nn.silu(gate) * val) @ w_down


def setup_inputs(seed: int = 0) -> dict:
    key = jax.random.key(seed)
    ks = jax.random.split(key, 32)
    D = D_MODEL
    f32 = jnp.float32
    nrm = lambda k, shape, s: jax.random.normal(k, shape, f32) * s
    x = nrm(ks[0], (BATCH, SEQ, D), 1.0)
    c = nrm(ks[1], (BATCH, D), 1.0)
    offsets = jax.random.randint(ks[2], (BATCH, 1), 0, 4096, dtype=jnp.int32)
    positions = (offsets + jnp.arange(SEQ, dtype=jnp.int32)[None, :]).astype(jnp.int32)
    n_idx = jnp.arange(SSM_STATE, dtype=f32)
    return {
        "x": x,
        "c": c,
        "positions": positions,
        "ada_w": nrm(ks[3], (DEPTH, D, 6 * D), 0.5 * D ** -0.5),
        "ada_b": nrm(ks[4], (DEPTH, 6 * D), 0.02),
        "norm_mix": 1.0 + nrm(ks[5], (DEPTH, D), 0.02),
        "norm_ffn": 1.0 + nrm(ks[6], (DEPTH, D), 0.02),
        "attn_w_in": nrm(ks[7], (N_ATTN, D, IN_COLS), D ** -0.5),
        "attn_q_gain": 1.0 + nrm(ks[8], (N_ATTN, HEAD_DIM), 0.02),
        "attn_k_gain": 1.0 + nrm(ks[9], (N_ATTN, HEAD_DIM), 0.02),
        "attn_w_out": nrm(ks[10], (N_ATTN, D, D), D ** -0.5),
        "ssm_a_re": -0.5 * jnp.exp(nrm(ks[11], (N_SSM, N_GROUPS, SSM_STATE), 0.05)),
        "ssm_a_im": math.pi * n_idx + nrm(ks[12], (N_SSM, N_GROUPS, SSM_STATE), 0.05),
        "ssm_log_dt": jax.random.uniform(ks[13], (N_SSM, N_GROUPS), f32, math.log(1e-3), math.log(1e-1)),
        "ssm_b_re": nrm(ks[14], (N_SSM, N_GROUPS, SSM_STATE, SSM_GROUP), (2 * SSM_GROUP) ** -0.5),
        "ssm_b_im": nrm(ks[15], (N_SSM, N_GROUPS, SSM_STATE, SSM_GROUP), (2 * SSM_GROUP) ** -0.5),
        "ssm_c_re": nrm(ks[16], (N_SSM, N_GROUPS, SSM_GROUP, SSM_STATE), SSM_STATE ** -0.5),
        "ssm_c_im": nrm(ks[17], (N_SSM, N_GROUPS, SSM_GROUP, SSM_STATE), SSM_STATE ** -0.5),
        "ssm_d": nrm(ks[18], (N_SSM, D), 1.0),
        "ssm_w_glu": nrm(ks[19], (N_SSM, D, 2 * D), D ** -0.5),
        "ffn_w_up": nrm(ks[20], (DEPTH, D, 2 * D_FF), D ** -0.5),
        "ffn_conv_w": nrm(ks[21], (DEPTH, CONV_W, 2 * D_FF), CONV_W ** -0.5),
        "ffn_conv_b": nrm(ks[22], (DEPTH, 2 * D_FF), 0.02),
        "ffn_w_down": nrm(ks[23], (DEPTH, D_FF, D), D_FF ** -0.5),
    }


def reference(x, c, positions, ada_w, ada_b, norm_mix, norm_ffn, attn_w_in, attn_q_gain,
              attn_k_gain, attn_w_out, ssm_a_re, ssm_a_im, ssm_log_dt, ssm_b_re, ssm_b_im,
              ssm_c_re, ssm_c_im, ssm_d, ssm_w_glu, ffn_w_up, ffn_conv_w, ffn_conv_b, ffn_w_down):
    cond = jax.nn.silu(c)
    for i in range(DEPTH):
        mod = cond @ ada_w[i] + ada_b[i]
        sh_m, sc_m, g_m, sh_f, sc_f, g_f = [m[:, None, :] for m in jnp.split(mod, 6, axis=-1)]
        h = rms_norm(x, norm_mix[i]) * (1.0 + sc_m) + sh_m
        j = i // N_MIXERS
        if i % N_MIXERS == 0:
            y = dsa_attention(h, positions, attn_w_in[j], attn_q_gain[j], attn_k_gain[j], attn_w_out[j])
        else:
            y = s5_mixer(h, ssm_a_re[j], ssm_a_im[j], ssm_log_dt[j], ssm_b_re[j], ssm_b_im[j],
                         ssm_c_re[j], ssm_c_im[j], ssm_d[j], ssm_w_glu[j])
        x = x + g_m * y
        h = rms_norm(x, norm_ffn[i]) * (1.0 + sc_f) + sh_f
        x = x + g_f * conv_ffn(h, ffn_w_up[i], ffn_conv_w[i], ffn_conv_b[i], ffn_w_down[i])
    return x
```

```python
import numpy as np
from contextlib import ExitStack
import concourse.bass as bass
import concourse.mybir as mybir
from concourse.bass_utils import run_bass_kernel_spmd

F32 = mybir.dt.float32
BF16 = mybir.dt.bfloat16
I32 = mybir.dt.int32
ALU = mybir.AluOpType
AF = mybir.ActivationFunctionType
AX = mybir.AxisListType

SAME_SYNC = True


class BufState:
    __slots__ = ("wr", "rd", "dsem", "dcnt", "persist")

    def __init__(self):
        self.wr = None
        self.rd = {}
        self.dsem = None
        self.dcnt = 0
        self.persist = False


class Buf:
    __slots__ = ("name", "t", "st")

    def __init__(self, name, t, parent=None):
        self.name = name
        self.t = t
        self.st = parent.st if parent is not None else BufState()

    def view(self, t):
        return Buf(self.name, t, parent=self)

    def __getitem__(self, k):
        return self.t[k]


class Prog:
    ENG = ("pe", "act", "dve", "pool", "sp")

    def __init__(self, nc):
        self.nc = nc
        self.es = ExitStack()
        self.sems = []
        self.ops = {k: [] for k in self.ENG}
        self.ecnt = {k: 0 for k in self.ENG}
        self.seen = {k: {} for k in self.ENG}
        self.esem = {}
        for k in ("pe", "act", "dve", "pool"):
            self.esem[k] = self._newsem("e_" + k)
        self.outs = []
        self.nbuf = 0
        self.pes = None
        self.dpool = []
        self.phase_states = []
        self.all_states = []
        self.barrier = {}
        self.nphase = 0

    def _newsem(self, name):
        s = self.es.enter_context(self.nc.semaphore(name))
        self.sems.append(s)
        return len(self.sems) - 1

    def sbuf(self, name, shape, dt):
        name = f"{name}_p{self.nphase}"
        t = (self.pes or self.es).enter_context(self.nc.sbuf_tensor(name, list(shape), dt))
        if getattr(self, "verbose", False):
            print("sbuf", name, shape, "remaining", self.nc.sbuf_bytes_remaining)
        return Buf(name, t)

    def psum(self, name, shape, dt=F32):
        name = f"{name}_p{self.nphase}"
        t = (self.pes or self.es).enter_context(self.nc.psum_tensor(name, list(shape), dt))
        return Buf(name, t)

    def dram(self, name, shape, dt, kind):
        t = self.nc.dram_tensor(name, list(shape), dt, kind=kind).ap()
        b = Buf(name, t)
        b.st.persist = True
        if kind == "ExternalOutput":
            self.outs.append(b)
        return b

    def alias(self, name, t):
        return Buf(name, t)

    def _deps(self, reads, writes):
        d = {}

        def add(tok):
            if tok is None:
                return
            s, v = tok
            if d.get(s, 0) < v:
                d[s] = v

        for s_, v_ in self.barrier.items():
            add((s_, v_))
        for b in reads:
            add(b.st.wr)
        for b in writes:
            add(b.st.wr)
            for s, v in b.st.rd.items():
                add((s, v))
        return d

    def _waits(self, eng, d):
        seen = self.seen[eng]
        own = self.esem.get(eng)
        w = []
        for s, v in d.items():
            if s == own and (eng == "pe" or not SAME_SYNC):
                continue
            if seen.get(s, 0) >= v:
                continue
            seen[s] = v
            w.append((s, v))
        return w

    def op(self, eng, fn, reads=(), writes=()):
        d = self._deps(reads, writes)
        w = self._waits(eng, d)
        self.ecnt[eng] += 1
        s = self.esem[eng]
        tok = (s, self.ecnt[eng])
        self.ops[eng].append((w, fn, s, 1))
        for b in reads:
            if b.st.rd.get(s, 0) < tok[1]:
                b.st.rd[s] = tok[1]
        for b in writes:
            b.st.wr = tok
            b.st.rd = {}
        return tok

    def dma(self, q, out_ap, in_ap, src, dst, **kw):
        d = self._deps([src], [dst])
        w = self._waits(q, d)
        ds = dst.st
        if ds.dsem is None:
            if self.dpool and not ds.persist:
                ds.dsem, ds.dcnt = self.dpool.pop()
            else:
                ds.dsem = self._newsem("d_" + dst.name)
            if not ds.persist:
                self.phase_states.append(ds)
            self.all_states.append(ds)
        ds.dcnt += 16
        tok = (ds.dsem, ds.dcnt)

        def fn(e, out_ap=out_ap, in_ap=in_ap, kw=kw):
            return e.dma_start(out=out_ap, in_=in_ap, **kw)

        self.ops[q].append((w, fn, ds.dsem, 16))
        if src.st.rd.get(tok[0], 0) < tok[1]:
            src.st.rd[tok[0]] = tok[1]
        ds.wr = tok
        ds.rd = {}
        return tok

    def _emit_block(self, fin=()):
        nc = self.nc
        sems = self.sems
        ops = self.ops

        def emit(e, lst, extra=()):
            for (w, fn, s, inc) in lst:
                for (ws, wv) in w:
                    e.wait_ge(sems[ws], wv)
                ins = fn(e)
                ins.then_inc(sems[s], inc)
            for (ws, wv) in extra:
                e.wait_ge(sems[ws], wv)

        with nc.Block() as block:
            @block.tensor
            def _(e):
                emit(e, ops["pe"])

            @block.scalar
            def _(e):
                emit(e, ops["act"])

            @block.vector
            def _(e):
                emit(e, ops["dve"])

            @block.gpsimd
            def _(e):
                emit(e, ops["pool"])

            @block.sync
            def _(e):
                emit(e, ops["sp"], fin)
        self.ops = {k: [] for k in self.ENG}

    def begin_phase(self):
        self.pes = ExitStack()
        self.phase_states = []

    def end_phase(self):
        self._emit_block()
        self.pes.close()
        self.pes = None
        self.nphase += 1
        bar = {}
        for k, s in self.esem.items():
            if self.ecnt[k]:
                bar[s] = self.ecnt[k]
        for ds in self.all_states:
            if ds.dsem is not None and ds.dcnt:
                bar[ds.dsem] = max(bar.get(ds.dsem, 0), ds.dcnt)
        self.barrier = bar
        for ds in self.phase_states:
            self.dpool.append((ds.dsem, ds.dcnt))
        self.all_states = [ds for ds in self.all_states if ds.persist]
        self.phase_states = []

    def finish(self):
        fin = []
        for b in self.outs:
            if b.st.dsem is not None:
                fin.append((b.st.dsem, b.st.dcnt))
        self._emit_block(fin)
        if self.pes is not None:
            self.pes.close()
        self.es.close()
        return self.nc

    def count(self):
        return {k: len(v) for k, v in self.ops.items()}


D = 1024
TC = 4096
NT = 512
NTILES = TC // NT
FF = 2816
FF2 = 5632
EPS = 1e-6
NEG = -1.0e30


def col_layout(v, nch):
    return np.ascontiguousarray(np.asarray(v).reshape(nch, 128).T)


class Common:
    def __init__(self, p, nstage=2):
        self.p = p
        self.nstage = nstage
        self.stage = [p.sbuf(f"wstage{i}", [128, 1024], F32) for i in range(nstage)]
        self.stage_i = 0
        self.cast_i = 0
        self.ones_bf = p.sbuf("ones_bf", [128, 128], BF16)
        p.op("pool", lambda e: e.memset(self.ones_bf[:], 1.0), writes=[self.ones_bf])
        self.dq = 0

    def dmaq(self):
        self.dq += 1
        return ("sp", "pool")[self.dq % 2]

    def load_cast(self, dst, dst_ap_fn, src, src_rows_ap_fn, ncols):
        p = self.p
        for c0 in range(0, ncols, 1024):
            c1 = min(ncols, c0 + 1024)
            st = self.stage[self.stage_i % self.nstage]
            self.stage_i += 1
            p.dma("sp", st[:, 0:c1 - c0], src_rows_ap_fn(c0, c1), src, st)
            eng = ("pool", "act", "dve")[self.cast_i % 3]
            self.cast_i += 1
            if eng == "act":
                p.op("act", lambda e, o=dst_ap_fn(c0, c1), i=st[:, 0:c1 - c0]: e.copy(o, i), reads=[st], writes=[dst])
            else:
                p.op(eng, lambda e, o=dst_ap_fn(c0, c1), i=st[:, 0:c1 - c0]: e.tensor_copy(o, i), reads=[st], writes=[dst])

    def emit_mod(self, cT, adaw, adab, nch, ps):
        p = self.p
        cond = p.sbuf("cond", [128, 8], F32)
        bcol = p.sbuf("adab_sb", [128, nch], F32)
        mod = p.sbuf("modcol", [128, nch], F32)
        p.dma("sp", cond[:], cT[:], cT, cond)
        p.dma("sp", bcol[:], adab[:], adab, bcol)
        p.op("act", lambda e: e.activation(out=cond[:], in_=cond[:], func=AF.Silu), reads=[cond], writes=[cond])
        for f in range(nch):
            st = self.stage[self.stage_i % self.nstage]
            self.stage_i += 1
            src = adaw.t[:, f * 128:(f + 1) * 128].rearrange("(k p) c -> p k c", p=128)
            p.dma(self.dmaq(), st[:, 0:1024].rearrange("p (k c) -> p k c", k=8), src, adaw, st)
            for k in range(8):
                p.op("pe", lambda e, st=st, k=k, f=f: e.matmul(ps[:, f:f + 1], st[:, k * 128:(k + 1) * 128], cond[:, k:k + 1],
                                                            start=(k == 0), stop=(k == 7)),
                     reads=[st, cond], writes=[ps])
        p.op("dve", lambda e: e.tensor_tensor(mod[:], ps[:, 0:nch], bcol[:], ALU.add), reads=[ps, bcol], writes=[mod])
        return mod

    def emit_gm(self, mod, sc_c0, gnorm_dram):
        p = self.p
        g = p.sbuf("gnorm_sb", [128, 8], F32)
        gm = p.sbuf("gm_sb", [128, 8], F32)
        p.dma("sp", g[:], gnorm_dram[:], gnorm_dram, g)
        p.op("dve", lambda e: e.tensor_scalar(gm[:], mod[:, sc_c0:sc_c0 + 8], 1.0, None, ALU.add), reads=[mod], writes=[gm])
        p.op("dve", lambda e: e.tensor_tensor(gm[:], gm[:], g[:], ALU.mult), reads=[gm, g], writes=[gm])
        return gm

    def alloc_norm(self, sq=None):
        p = self.p
        self.sq = sq if sq is not None else p.sbuf("sq", [128, 8, NT], BF16)
        self.rstd = p.sbuf("rstd", [128, NT], F32)
        self.ntmp = [p.sbuf(f"ntmp{i}", [128, NT], F32) for i in range(2)]
        self.epscol = p.sbuf("epscol", [128, 1], F32)
        p.op("pool", lambda e: e.memset(self.epscol[:], EPS), writes=[self.epscol])

    def emit_norm(self, xt, n, gm, mod, sh_c0, hT, ps):
        p = self.p
        sq, rstd = self.sq, self.rstd
        p.op("act", lambda e: e.activation(out=sq[:, 0:8, 0:n], in_=xt[:, 0:8, 0:n], func=AF.Square), reads=[xt], writes=[sq])
        for k in range(8):
            p.op("pe", lambda e, k=k: e.matmul(ps[:, 0:n], self.ones_bf[:], sq[:, k, 0:n], start=(k == 0), stop=(k == 7)),
                 reads=[sq, self.ones_bf], writes=[ps])
        p.op("act", lambda e: e.activation(out=rstd[:, 0:n], in_=ps[:, 0:n], func=AF.Sqrt, bias=self.epscol[:], scale=1.0 / D),
             reads=[ps, self.epscol], writes=[rstd])
        p.op("dve", lambda e: e.reciprocal(rstd[:, 0:n], rstd[:, 0:n]), reads=[rstd], writes=[rstd])
        for k in range(8):
            tmp = self.ntmp[k % 2]
            p.op("dve", lambda e, k=k, tmp=tmp: e.scalar_tensor_tensor(tmp[:, 0:n], xt[:, k, 0:n], gm[:, k:k + 1], rstd[:, 0:n],
                                                                       ALU.mult, ALU.mult),
                 reads=[xt, gm, rstd], writes=[tmp])
            p.op("act", lambda e, k=k, tmp=tmp: e.activation(out=hT[:, k, 0:n], in_=tmp[:, 0:n], func=AF.Identity,
                                                             bias=mod[:, sh_c0 + k:sh_c0 + k + 1], scale=1.0),
                 reads=[tmp, mod], writes=[hT])


def build_ffn(p=None, io=None):
    standalone = p is None
    if standalone:
        nc = bass.Bass("TRN2", target_bir_lowering=False)
        p = Prog(nc)
    p.begin_phase()
    xT = io["xT"] if io is not None else p.dram("xT", [D, 2 + TC], F32, "ExternalInput")
    cT = io["cT"] if io is not None else p.dram("cT", [128, 8], F32, "ExternalInput")
    adaw = io["adaw"] if io is not None else p.dram("adaw", [D, 3 * D], F32, "ExternalInput")
    adab = io["adab"] if io is not None else p.dram("adab", [128, 24], F32, "ExternalInput")
    gnorm = io["gnorm"] if io is not None else p.dram("gnorm", [128, 8], F32, "ExternalInput")
    w_up = io["w_up"] if io is not None else p.dram("w_up", [D, FF2], F32, "ExternalInput")
    w_down = io["w_down"] if io is not None else p.dram("w_down", [FF, D], F32, "ExternalInput")
    conv_w = io["conv_w"] if io is not None else p.dram("conv_w", [128, 44, 3], F32, "ExternalInput")
    conv_b = io["conv_b"] if io is not None else p.dram("conv_b", [128, 44], F32, "ExternalInput")
    hflag = io["hflag"] if io is not None else p.dram("hflag", [128, 1], F32, "ExternalInput")
    yT = io["yT"] if io is not None else p.dram("yT", [D, TC], F32, "ExternalOutput")

    cm = Common(p)
    ps_ss = p.psum("ps_ss", [128, 512])
    ps_v = [p.psum(f"ps_v{i}", [128, 512]) for i in range(2)]
    ps_g = [p.psum(f"ps_g{i}", [128, 512]) for i in range(2)]
    ps_d = [p.psum(f"ps_d{i}", [128, 512]) for i in range(2)]

    mod = cm.emit_mod(cT, adaw, adab, 24, ps_ss)
    gm = cm.emit_gm(mod, 8, gnorm)
    actT = p.sbuf("actT", [128, 22, NT], BF16)
    cm.alloc_norm(sq=actT)

    wup = p.sbuf("wup", [128, 8, FF2], BF16)
    wdn = p.sbuf("wdn", [128, 22, D], BF16)
    cw = p.sbuf("cw", [128, 44, 3], F32)
    cb = p.sbuf("cb", [128, 44], F32)
    hf = p.sbuf("hf", [128, 1], F32)
    p.dma("sp", cw[:], conv_w[:], conv_w, cw)
    p.dma("sp", cb[:], conv_b[:], conv_b, cb)
    p.dma("sp", hf[:], hflag[:], hflag, hf)
    for k in range(8):
        cm.load_cast(wup, lambda c0, c1, k=k: wup[:, k, c0:c1], w_up,
                     lambda c0, c1, k=k: w_up.t[k * 128:(k + 1) * 128, c0:c1], FF2)
    for m in range(22):
        cm.load_cast(wdn, lambda c0, c1, m=m: wdn[:, m, c0:c1], w_down,
                     lambda c0, c1, m=m: w_down.t[m * 128:(m + 1) * 128, c0:c1], D)

    xt = p.sbuf("xt", [128, 8, NT], F32)
    xh = p.sbuf("xh", [128, 8, 2], F32)
    hT = p.sbuf("hT", [128, 8, NT], BF16)
    carry = p.sbuf("carry", [128, 44, 2], F32)
    Ev = [p.sbuf(f"Ev{i}", [128, NT + 2], F32) for i in range(2)]
    Eg = [p.sbuf(f"Eg{i}", [128, NT + 2], F32) for i in range(2)]
    cv = [p.sbuf("cv0", [128, NT], F32)] * 2
    cg = [p.sbuf("cg0", [128, NT], F32)] * 2

    xTv = xT.t.rearrange("(k p) t -> p k t", p=128)
    yTv = yT.t.rearrange("(k p) t -> p k t", p=128)

    p.dma("sp", xh[:], xTv[:, :, 0:2], xT, xh)
    cm.emit_norm(xh, 2, gm, mod, 0, hT, ps_ss)
    for m in range(44):
        ps = ps_v[m % 2]
        for k in range(8):
            p.op("pe", lambda e, m=m, k=k, ps=ps: e.matmul(ps[:, 0:2], wup[:, k, m * 128:(m + 1) * 128], hT[:, k, 0:2],
                                                          start=(k == 0), stop=(k == 7)),
                 reads=[wup, hT], writes=[ps])
        p.op("dve", lambda e, m=m, ps=ps: e.tensor_scalar(carry[:, m, :], ps[:, 0:2], hf[:, 0:1], None, ALU.mult),
             reads=[ps, hf], writes=[carry])

    def conv(m, ps, E, cvb):
        p.op("pool", lambda e: e.tensor_copy(E[:, 0:2], carry[:, m, :]), reads=[carry], writes=[E])
        p.op("act", lambda e: e.copy(E[:, 2:NT + 2], ps[:, :]), reads=[ps], writes=[E])
        p.op("pool", lambda e: e.tensor_copy(carry[:, m, :], E[:, NT:NT + 2]), reads=[E], writes=[carry])
        p.op("pool", lambda e: e.tensor_scalar(cvb[:], E[:, 2:NT + 2], cw[:, m, 2:3], cb[:, m:m + 1], ALU.mult, ALU.add),
             reads=[E, cw, cb], writes=[cvb])
        p.op("dve", lambda e: e.scalar_tensor_tensor(cvb[:], E[:, 1:NT + 1], cw[:, m, 1:2], cvb[:], ALU.mult, ALU.add),
             reads=[E, cw, cvb], writes=[cvb])
        p.op("dve", lambda e: e.scalar_tensor_tensor(cvb[:], E[:, 0:NT], cw[:, m, 0:1], cvb[:], ALU.mult, ALU.add),
             reads=[E, cw, cvb], writes=[cvb])

    for t in range(NTILES):
        p.dma("sp", xt[:], xTv[:, :, 2 + t * NT:2 + (t + 1) * NT], xT, xt)
        cm.emit_norm(xt, NT, gm, mod, 0, hT, ps_ss)
        for m in range(22):
            i = m % 2
            for (mm, ps) in ((m, ps_v[i]), (m + 22, ps_g[i])):
                for k in range(8):
                    p.op("pe", lambda e, mm=mm, k=k, ps=ps: e.matmul(ps[:, :], wup[:, k, mm * 128:(mm + 1) * 128], hT[:, k, :],
                                                                    start=(k == 0), stop=(k == 7)),
                         reads=[wup, hT], writes=[ps])
            conv(m, ps_v[i], Ev[i], cv[i])
            conv(m + 22, ps_g[i], Eg[i], cg[i])
            p.op("act", lambda e, i=i: e.activation(out=Eg[i][:, 0:NT], in_=cg[i][:], func=AF.Silu), reads=[cg[i]], writes=[Eg[i]])
            p.op("dve", lambda e, i=i, m=m: e.tensor_tensor(actT[:, m, :], Eg[i][:, 0:NT], cv[i][:], ALU.mult),
                 reads=[Eg[i], cv[i]], writes=[actT])
        for n in range(8):
            ps = ps_d[n % 2]
            for m in range(22):
                p.op("pe", lambda e, n=n, m=m, ps=ps: e.matmul(ps[:, :], wdn[:, m, n * 128:(n + 1) * 128], actT[:, m, :],
                                                              start=(m == 0), stop=(m == 21)),
                     reads=[wdn, actT], writes=[ps])
            p.op("dve", lambda e, n=n, ps=ps: e.scalar_tensor_tensor(xt[:, n, :], ps[:, :], mod[:, 16 + n:17 + n], xt[:, n, :],
                                                                     ALU.mult, ALU.add),
                 reads=[ps, mod, xt], writes=[xt])
        p.dma("sp", yTv[:, :, t * NT:(t + 1) * NT], xt[:], xt, yT)
    print("ffn ops", p.count())
    if standalone:
        return p.finish()
    p.end_phase()


SEQ = 16384
OWN0 = 0
CT = 256
PI = float(np.pi)


def emit_range_reduce(p, eng, x, tmp_i, tmp_f, n):
    p.op(eng, lambda e: e.tensor_scalar(tmp_f[:, 0:n], x[:, 0:n], 1.0 / (2 * PI), 0.5, ALU.mult, ALU.add), reads=[x], writes=[tmp_f])
    p.op(eng, lambda e: e.tensor_copy(tmp_i[:, 0:n], tmp_f[:, 0:n]), reads=[tmp_f], writes=[tmp_i])
    p.op(eng, lambda e: e.tensor_copy(tmp_f[:, 0:n], tmp_i[:, 0:n]), reads=[tmp_i], writes=[tmp_f])
    p.op(eng, lambda e: e.scalar_tensor_tensor(x[:, 0:n], tmp_f[:, 0:n], -2 * PI, x[:, 0:n], ALU.mult, ALU.add), reads=[tmp_f, x], writes=[x])
    p.op(eng, lambda e: e.tensor_scalar(tmp_f[:, 0:n], x[:, 0:n], PI, -2 * PI, ALU.is_gt, ALU.mult), reads=[x], writes=[tmp_f])
    p.op(eng, lambda e: e.tensor_tensor(x[:, 0:n], x[:, 0:n], tmp_f[:, 0:n], ALU.add), reads=[x, tmp_f], writes=[x])
    p.op(eng, lambda e: e.tensor_scalar(tmp_f[:, 0:n], x[:, 0:n], -PI, 2 * PI, ALU.is_lt, ALU.mult), reads=[x], writes=[tmp_f])
    p.op(eng, lambda e: e.tensor_tensor(x[:, 0:n], x[:, 0:n], tmp_f[:, 0:n], ALU.add), reads=[x, tmp_f], writes=[x])


def build_scan(p=None, io=None):
    standalone = p is None
    if standalone:
        nc = bass.Bass("TRN2", target_bir_lowering=False)
        p = Prog(nc)
    p.begin_phase()
    xT = io["xT"] if io is not None else p.dram("xT", [D, SEQ], F32, "ExternalInput")
    cT = io["cT"] if io is not None else p.dram("cT", [128, 8], F32, "ExternalInput")
    adaw = io["adaw"] if io is not None else p.dram("adaw", [D, 512], F32, "ExternalInput")
    adab = io["adab"] if io is not None else p.dram("adab", [128, 4], F32, "ExternalInput")
    gnorm = io["gnorm"] if io is not None else p.dram("gnorm", [128, 2], F32, "ExternalInput")
    prmR = io["prmR"] if io is not None else p.dram("prmR", [128, 5, 128], F32, "ExternalInput")
    prmS = io["prmS"] if io is not None else p.dram("prmS", [128, 3, 16], F32, "ExternalInput")
    cS = io["cS"] if io is not None else p.dram("cS", [128, 16, 16], F32, "ExternalInput")
    dsk = io["dsk"] if io is not None else p.dram("dsk", [128, 2], F32, "ExternalInput")
    consts = io["consts"] if io is not None else p.dram("consts", [128, 10 + CT], F32, "ExternalInput")
    yT = io["yT"] if io is not None else p.dram("yT", [256, SEQ], F32, "ExternalOutput")

    cm = Common(p)
    ps_ss = p.psum("ps_ss", [128, 512])
    psA = [p.psum(f"psA{i}", [128, 512]) for i in range(2)]
    psB = [p.psum(f"psB{i}", [128, 512]) for i in range(2)]
    psY = [p.psum(f"psY{i}", [128, 512]) for i in range(2)]

    mod = cm.emit_mod(cT, adaw, adab, 4, ps_ss)
    g = p.sbuf("gn", [128, 2], F32)
    gm = p.sbuf("gm", [128, 2], F32)
    p.dma("sp", g[:], gnorm[:], gnorm, g)
    p.op("dve", lambda e: e.tensor_scalar(gm[:], mod[:, 2:4], 1.0, None, ALU.add), reads=[mod], writes=[gm])
    p.op("dve", lambda e: e.tensor_tensor(gm[:], gm[:], g[:], ALU.mult), reads=[gm, g], writes=[gm])

    cst = p.sbuf("cst", [128, 10 + CT], F32)
    p.dma("sp", cst[:], consts[:], consts, cst)
    sgn = lambda: cst[:, 0:1]
    iota = lambda: cst[:, 10:10 + CT]
    dcol = p.sbuf("dcol", [128, 2], F32)
    p.dma("sp", dcol[:], dsk[:], dsk, dcol)

    R = p.sbuf("prmR_sb", [128, 5, 128], F32)
    p.dma("sp", R[:], prmR[:], prmR, R)
    T_ = [p.sbuf(f"Rt{i}", [128, 128], F32) for i in range(10)]
    Ti = p.sbuf("Rti", [128, CT], I32)
    Tf = p.sbuf("Rtf", [128, CT], F32)
    dt, ar, ai, mag, cs, sn, lr, li, t8, t9 = T_

    def V(eng, fn, reads, writes):
        p.op(eng, fn, reads=reads, writes=writes)

    def setup_abar(are, aim, ldt, n):
        pass

    are, aim, ldt = (lambda: R[:, 0, :]), (lambda: R[:, 1, :]), (lambda: R[:, 2, :])
    bre, bim = (lambda: R[:, 3, :]), (lambda: R[:, 4, :])
    V("act", lambda e: e.activation(out=dt[:], in_=ldt(), func=AF.Exp), [R], [dt])
    V("dve", lambda e: e.tensor_tensor(ar[:], are(), dt[:], ALU.mult), [R, dt], [ar])
    V("dve", lambda e: e.tensor_tensor(ai[:], aim(), dt[:], ALU.mult), [R, dt], [ai])
    V("act", lambda e: e.activation(out=mag[:], in_=ar[:], func=AF.Exp), [ar], [mag])
    V("dve", lambda e: e.tensor_scalar(cs[:], ai[:], PI / 2, None, ALU.add), [ai], [cs])
    V("dve", lambda e: e.tensor_copy(sn[:], ai[:]), [ai], [sn])
    emit_range_reduce(p, "dve", cs, Ti, Tf, 128)
    emit_range_reduce(p, "dve", sn, Ti, Tf, 128)
    V("act", lambda e: e.activation(out=cs[:], in_=cs[:], func=AF.Sin), [cs], [cs])
    V("act", lambda e: e.activation(out=sn[:], in_=sn[:], func=AF.Sin), [sn], [sn])
    V("dve", lambda e: e.tensor_tensor(lr[:], mag[:], cs[:], ALU.mult), [mag, cs], [lr])
    V("dve", lambda e: e.tensor_tensor(li[:], mag[:], sn[:], ALU.mult), [mag, sn], [li])
    V("dve", lambda e: e.tensor_tensor(dt[:], are(), are(), ALU.mult), [R], [dt])
    V("dve", lambda e: e.tensor_tensor(t8[:], aim(), aim(), ALU.mult), [R], [t8])
    V("dve", lambda e: e.tensor_tensor(dt[:], dt[:], t8[:], ALU.add), [dt, t8], [dt])
    V("dve", lambda e: e.reciprocal(dt[:], dt[:]), [dt], [dt])
    V("dve", lambda e: e.tensor_scalar(lr[:], lr[:], -1.0, None, ALU.add), [lr], [lr])
    V("dve", lambda e: e.tensor_tensor(t8[:], lr[:], are(), ALU.mult), [lr, R], [t8])
    V("dve", lambda e: e.tensor_tensor(t9[:], li[:], aim(), ALU.mult), [li, R], [t9])
    V("dve", lambda e: e.tensor_tensor(t8[:], t8[:], t9[:], ALU.add), [t8, t9], [t8])
    V("dve", lambda e: e.tensor_tensor(cs[:], t8[:], dt[:], ALU.mult), [t8, dt], [cs])
    V("dve", lambda e: e.tensor_tensor(t8[:], li[:], are(), ALU.mult), [li, R], [t8])
    V("dve", lambda e: e.tensor_tensor(t9[:], lr[:], aim(), ALU.mult), [lr, R], [t9])
    V("dve", lambda e: e.tensor_tensor(t8[:], t8[:], t9[:], ALU.subtract), [t8, t9], [t8])
    V("dve", lambda e: e.tensor_tensor(sn[:], t8[:], dt[:], ALU.mult), [t8, dt], [sn])
    V("dve", lambda e: e.tensor_tensor(t8[:], cs[:], bre(), ALU.mult), [cs, R], [t8])
    V("dve", lambda e: e.tensor_tensor(t9[:], sn[:], bim(), ALU.mult), [sn, R], [t9])
    V("dve", lambda e: e.tensor_tensor(ar[:], t8[:], t9[:], ALU.subtract), [t8, t9], [ar])
    V("dve", lambda e: e.tensor_tensor(t8[:], cs[:], bim(), ALU.mult), [cs, R], [t8])
    V("dve", lambda e: e.tensor_tensor(t9[:], sn[:], bre(), ALU.mult), [sn, R], [t9])
    V("dve", lambda e: e.tensor_tensor(ai[:], t8[:], t9[:], ALU.add), [t8, t9], [ai])
    Wa = p.sbuf("Wa", [128, 16, 128], BF16)
    Wb = p.sbuf("Wb", [128, 16, 128], BF16)
    for gg in range(16):
        ck, gl = gg // 8, gg % 8
        rm = cst[:, 1 + gl:2 + gl]
        sl = slice(ck * 64, ck * 64 + 64)
        V("dve", lambda e, gg=gg, rm=rm, sl=sl: e.tensor_scalar(Wa[:, gg, 0:64], ar[:, sl], rm, None, ALU.mult), [ar, cst], [Wa])
        V("dve", lambda e, gg=gg, rm=rm, sl=sl: e.tensor_scalar(Wa[:, gg, 64:128], ai[:, sl], rm, None, ALU.mult), [ai, cst], [Wa])
        V("pool", lambda e, gg=gg, rm=rm, sl=sl: e.tensor_scalar(Wb[:, gg, 0:64], ai[:, sl], rm, None, ALU.mult), [ai, cst], [Wb])
        V("pool", lambda e, gg=gg, rm=rm, sl=sl: e.tensor_scalar(Wb[:, gg, 64:128], ar[:, sl], rm, None, ALU.mult), [ar, cst], [Wb])
    cSs = p.sbuf("cS_sb", [128, 16, 16], F32)
    p.dma("sp", cSs[:], cS[:], cS, cSs)
    Hpad = p.sbuf("Hpad", [128, 16, 128], BF16)
    V("pool", lambda e: e.memset(Hpad[:], 0.0), [], [Hpad])
    for gg in range(16):
        gl = gg % 8
        V("dve", lambda e, gg=gg, gl=gl: e.tensor_scalar(Hpad[:, gg, 16 * gl:16 * gl + 16], cSs[:, gg, :], sgn(), None, ALU.mult),
          [cSs, cst], [Hpad])

    S = p.sbuf("prmS_sb", [128, 3, 16], F32)
    p.dma("sp", S[:], prmS[:], prmS, S)
    dtS = p.sbuf("dtS", [128, 16], F32)
    rho = p.sbuf("rho", [128, 16], F32)
    th = p.sbuf("th", [128, 16], F32)
    V("act", lambda e: e.activation(out=dtS[:], in_=S[:, 2, :], func=AF.Exp), [S], [dtS])
    V("dve", lambda e: e.tensor_tensor(rho[:], S[:, 0, :], dtS[:], ALU.mult), [S, dtS], [rho])
    V("act", lambda e: e.activation(out=rho[:], in_=rho[:], func=AF.Exp), [rho], [rho])
    V("dve", lambda e: e.tensor_tensor(th[:], S[:, 1, :], dtS[:], ALU.mult), [S, dtS], [th])
    C1 = p.sbuf("C1", [128, 16, CT], F32)
    C2 = p.sbuf("C2", [128, 16, CT], F32)
    RB = p.sbuf("RB", [128, 16, CT], F32)
    ang = p.sbuf("ang", [128, CT], F32)
    for gg in range(16):
        V("dve", lambda e, gg=gg: e.tensor_scalar(ang[:], iota(), th[:, gg:gg + 1], None, ALU.mult), [cst, th], [ang])
        emit_range_reduce(p, "dve", ang, Ti, Tf, CT)
        V("act", lambda e, gg=gg: e.activation(out=C2[:, gg, :], in_=ang[:], func=AF.Sin, scale=1.0), [ang], [C2])
        V("pool", lambda e, gg=gg: e.tensor_scalar(C2[:, gg, :], C2[:, gg, :], sgn(), None, ALU.mult), [C2, cst], [C2])
        V("dve", lambda e, gg=gg: e.tensor_scalar(ang[:], iota(), th[:, gg:gg + 1], PI / 2, ALU.mult, ALU.add), [cst, th], [ang])
        emit_range_reduce(p, "dve", ang, Ti, Tf, CT)
        V("act", lambda e, gg=gg: e.activation(out=C1[:, gg, :], in_=ang[:], func=AF.Sin, scale=1.0), [ang], [C1])
        V("pool", lambda e, gg=gg: e.memset(RB[:, gg, :], 1.0), [], [RB])
        V("pool", lambda e, gg=gg: e.tensor_scalar(RB[:, gg, :], RB[:, gg, :], rho[:, gg:gg + 1], None, ALU.mult), [RB, rho], [RB])

    carry = p.sbuf("xcarry", [128, 16], F32)
    V("pool", lambda e: e.memset(carry[:], 0.0), [], [carry])
    xt = p.sbuf("xt", [128, 8, CT], F32)
    sq = p.sbuf("sq", [128, 8, CT], BF16)
    rstd = p.sbuf("rstd", [128, CT], F32)
    epscol = p.sbuf("epscol", [128, 1], F32)
    V("pool", lambda e: e.memset(epscol[:], EPS), [], [epscol])
    hf = p.sbuf("hf", [128, 2, CT], F32)
    hb = p.sbuf("hb", [128, 2, CT], BF16)
    W1 = [p.sbuf(f"W1_{i}", [128, CT], F32) for i in range(2)]
    W2 = [p.sbuf(f"W2_{i}", [128, CT], F32) for i in range(2)]
    ww = [p.sbuf(f"ww_{i}", [128, CT], F32) for i in range(2)]
    wsw = [p.sbuf(f"wsw_{i}", [128, CT], F32) for i in range(2)]
    xs = [p.sbuf(f"xs_{i}", [128, CT], F32) for i in range(2)]
    xb = [p.sbuf(f"xb_{i}", [128, CT], BF16) for i in range(8)]
    yo = p.sbuf("yo", [128, 2, CT], F32)
    g1 = p.sbuf("g1", [128, CT], F32)
    g2 = p.sbuf("g2", [128, CT], F32)
    xTv = xT.t.rearrange("(k p) t -> p k t", p=128)
    yTv = yT.t.rearrange("(k p) t -> p k t", p=128)
    GC = 0.7978845608028654

    for n in range(SEQ // CT):
        p.dma("sp", xt[:], xTv[:, :, n * CT:(n + 1) * CT], xT, xt)
        V("act", lambda e: e.activation(out=sq[:], in_=xt[:], func=AF.Square), [xt], [sq])
        for k in range(8):
            V("pe", lambda e, k=k: e.matmul(ps_ss[:, 0:CT], cm.ones_bf[:], sq[:, k, :], start=(k == 0), stop=(k == 7)), [sq, cm.ones_bf], [ps_ss])
        V("act", lambda e: e.activation(out=rstd[:], in_=ps_ss[:, 0:CT], func=AF.Sqrt, bias=epscol[:], scale=1.0 / D), [ps_ss, epscol], [rstd])
        V("dve", lambda e: e.reciprocal(rstd[:], rstd[:]), [rstd], [rstd])
        for ck in range(2):
            V("dve", lambda e, ck=ck: e.scalar_tensor_tensor(hf[:, ck, :], xt[:, OWN0 + ck, :], gm[:, ck:ck + 1], rstd[:], ALU.mult, ALU.mult), [xt, gm, rstd], [hf])
            V("act", lambda e, ck=ck: e.activation(out=hf[:, ck, :], in_=hf[:, ck, :], func=AF.Identity, bias=mod[:, ck:ck + 1], scale=1.0), [hf, mod], [hf])
            V("pool", lambda e, ck=ck: e.tensor_copy(hb[:, ck, :], hf[:, ck, :]), [hf], [hb])
        for ck in range(2):
            py = psY[ck]
            for gl in range(8):
                gg = ck * 8 + gl
                i = gg % 2
                pa, pb = psA[i], psB[i]
                V("pe", lambda e, gg=gg, ck=ck, pa=pa: e.matmul(pa[:, 0:CT], Wa[:, gg, :], hb[:, ck, :], start=True, stop=True), [Wa, hb], [pa])
                V("pe", lambda e, gg=gg, ck=ck, pb=pb: e.matmul(pb[:, 0:CT], Wb[:, gg, :], hb[:, ck, :], start=True, stop=True), [Wb, hb], [pb])
                V("dve", lambda e, gg=gg, i=i, pa=pa: e.tensor_tensor(W1[i][:], pa[:, 0:CT], C1[:, gg, :], ALU.mult), [pa, C1], [W1[i]])
                V("dve", lambda e, gg=gg, i=i, pb=pb: e.tensor_tensor(W2[i][:], pb[:, 0:CT], C2[:, gg, :], ALU.mult), [pb, C2], [W2[i]])
                V("pool", lambda e, i=i: e.tensor_tensor(W1[i][:], W1[i][:], W2[i][:], ALU.add), [W1[i], W2[i]], [W1[i]])
                V("dve", lambda e, gg=gg, i=i: e.tensor_tensor_scan(ww[i][:], RB[:, gg, :], W1[i][:], carry[:, gg:gg + 1], ALU.mult, ALU.add),
                  [RB, W1[i], carry], [ww[i]])
                V("act", lambda e, i=i: e.copy(wsw[i][0:64, :], ww[i][64:128, :]), [ww[i]], [wsw[i]])
                V("act", lambda e, i=i: e.copy(wsw[i][64:128, :], ww[i][0:64, :]), [ww[i]], [wsw[i]])
                V("dve", lambda e, gg=gg, i=i: e.tensor_tensor(xs[i][:], ww[i][:], C1[:, gg, :], ALU.mult), [ww[i], C1], [xs[i]])
                V("pool", lambda e, gg=gg, i=i: e.tensor_tensor(wsw[i][:], wsw[i][:], C2[:, gg, :], ALU.mult), [wsw[i], C2], [wsw[i]])
                V("pool", lambda e, i=i: e.tensor_tensor(xs[i][:], xs[i][:], wsw[i][:], ALU.subtract), [xs[i], wsw[i]], [xs[i]])
                V("pool", lambda e, gg=gg, i=i: e.tensor_copy(carry[:, gg:gg + 1], xs[i][:, CT - 1:CT]), [xs[i]], [carry])
                V("act", lambda e, gl=gl, i=i: e.copy(xb[gl][:], xs[i][:]), [xs[i]], [xb[gl]])
                V("pe", lambda e, gg=gg, gl=gl, py=py: e.matmul(py[:, 0:CT], Hpad[:, gg, :], xb[gl][:], start=(gl == 0), stop=(gl == 7)), [Hpad, xb[gl]], [py])
            V("dve", lambda e, ck=ck, py=py: e.scalar_tensor_tensor(yo[:, ck, :], hf[:, ck, :], dcol[:, ck:ck + 1], py[:, 0:CT], ALU.mult, ALU.add), [hf, dcol, py], [yo])
            V("dve", lambda e, ck=ck: e.tensor_tensor(g1[:], yo[:, ck, :], yo[:, ck, :], ALU.mult), [yo], [g1])
            V("dve", lambda e: e.tensor_scalar(g1[:], g1[:], 0.044715 * GC, GC, ALU.mult, ALU.add), [g1], [g1])
            V("dve", lambda e, ck=ck: e.tensor_tensor(g1[:], g1[:], yo[:, ck, :], ALU.mult), [g1, yo], [g1])
            V("act", lambda e: e.activation(out=g2[:], in_=g1[:], func=AF.Tanh), [g1], [g2])
            V("dve", lambda e: e.tensor_scalar(g2[:], g2[:], 1.0, 0.5, ALU.add, ALU.mult), [g2], [g2])
            V("dve", lambda e, ck=ck: e.tensor_tensor(yo[:, ck, :], yo[:, ck, :], g2[:], ALU.mult), [yo, g2], [yo])
        p.dma("sp", yTv[:, :, n * CT:(n + 1) * CT], yo[:], yo, yT)
    print("scan ops", p.count())
    if standalone:
        return p.finish()
    p.end_phase()


def build_glu(p=None, io=None):
    standalone = p is None
    if standalone:
        nc = bass.Bass("TRN2", target_bir_lowering=False)
        p = Prog(nc)
    p.begin_phase()
    xT = io["xT"] if io is not None else p.dram("xT", [D, TC], F32, "ExternalInput")
    gT = io["gT"] if io is not None else p.dram("gT", [D, TC], F32, "ExternalInput")
    cT = io["cT"] if io is not None else p.dram("cT", [128, 8], F32, "ExternalInput")
    adaw = io["adaw"] if io is not None else p.dram("adaw", [D, D], F32, "ExternalInput")
    adab = io["adab"] if io is not None else p.dram("adab", [128, 8], F32, "ExternalInput")
    w_glu = io["w_glu"] if io is not None else p.dram("w_glu", [D, 2 * D], F32, "ExternalInput")
    yT = io["yT"] if io is not None else p.dram("yT", [D, TC], F32, "ExternalOutput")
    cm = Common(p)
    ps_m = p.psum("ps_m", [128, 512])
    psa = [p.psum(f"psa{i}", [128, 512]) for i in range(2)]
    psb = [p.psum(f"psb{i}", [128, 512]) for i in range(2)]
    mod = cm.emit_mod(cT, adaw, adab, 8, ps_m)
    wg = p.sbuf("wg", [128, 8, 2 * D], BF16)
    for k in range(8):
        cm.load_cast(wg, lambda c0, c1, k=k: wg[:, k, c0:c1], w_glu, lambda c0, c1, k=k: w_glu.t[k * 128:(k + 1) * 128, c0:c1], 2 * D)
    xt = p.sbuf("xt", [128, 8, NT], F32)
    gt = p.sbuf("gt", [128, 8, NT], F32)
    gb = p.sbuf("gb", [128, 8, NT], BF16)
    sg = [p.sbuf(f"sg{i}", [128, NT], F32) for i in range(2)]
    xTv = xT.t.rearrange("(k p) t -> p k t", p=128)
    gTv = gT.t.rearrange("(k p) t -> p k t", p=128)
    yTv = yT.t.rearrange("(k p) t -> p k t", p=128)
    for t in range(TC // NT):
        sl = slice(t * NT, (t + 1) * NT)
        p.dma("sp", xt[:], xTv[:, :, sl], xT, xt)
        p.dma("pool", gt[:], gTv[:, :, sl], gT, gt)
        p.op("act", lambda e: e.copy(gb[:], gt[:]), reads=[gt], writes=[gb])
        for n in range(8):
            i = n % 2
            for k in range(8):
                p.op("pe", lambda e, n=n, k=k, i=i: e.matmul(psa[i][:, :], wg[:, k, n * 128:(n + 1) * 128], gb[:, k, :], start=(k == 0), stop=(k == 7)),
                     reads=[wg, gb], writes=[psa[i]])
            for k in range(8):
                p.op("pe", lambda e, n=n, k=k, i=i: e.matmul(psb[i][:, :], wg[:, k, D + n * 128:D + (n + 1) * 128], gb[:, k, :], start=(k == 0), stop=(k == 7)),
                     reads=[wg, gb], writes=[psb[i]])
            p.op("act", lambda e, i=i: e.activation(out=sg[i][:], in_=psb[i][:, :], func=AF.Sigmoid), reads=[psb[i]], writes=[sg[i]])
            p.op("dve", lambda e, i=i: e.tensor_tensor(sg[i][:], psa[i][:, :], sg[i][:], ALU.mult), reads=[psa[i], sg[i]], writes=[sg[i]])
            p.op("dve", lambda e, i=i, n=n: e.scalar_tensor_tensor(xt[:, n, :], sg[i][:], mod[:, n:n + 1], xt[:, n, :], ALU.mult, ALU.add),
                 reads=[sg[i], mod, xt], writes=[xt])
        p.dma("sp", yTv[:, :, sl], xt[:], xt, yT)
    print("glu ops", p.count())
    if standalone:
        return p.finish()
    p.end_phase()


TQ = 4096
INC = 3656


def attn_consts():
    c = np.zeros((128, 258), np.float32)
    inv = 10000.0 ** (-np.arange(32, dtype=np.float32) / 32)
    c[:, 0] = inv[np.arange(128) % 32]
    Rm = np.zeros((128, 128), np.float32)
    for i in range(128):
        if i % 64 < 32:
            Rm[i + 32, i] = -1.0
        else:
            Rm[i - 32, i] = 1.0
    c[:, 2:130] = Rm
    BD = np.zeros((128, 128), np.float32)
    BD[:64, :64] = 1
    BD[64:, 64:] = 1
    c[:, 130:258] = BD
    return c


def build_attn_p(p=None, io=None):
    standalone = p is None
    if standalone:
        nc = bass.Bass("TRN2", target_bir_lowering=False)
        p = Prog(nc)
    p.begin_phase()
    xT = io["xT"] if io is not None else p.dram("xT", [D, TQ], F32, "ExternalInput")
    pos = io["pos"] if io is not None else p.dram("pos", [1, TQ], I32, "ExternalInput")
    cT = io["cT"] if io is not None else p.dram("cT", [128, 8], F32, "ExternalInput")
    adaw = io["adaw"] if io is not None else p.dram("adaw", [D, 2 * D], F32, "ExternalInput")
    adab = io["adab"] if io is not None else p.dram("adab", [128, 16], F32, "ExternalInput")
    gnorm = io["gnorm"] if io is not None else p.dram("gnorm", [128, 8], F32, "ExternalInput")
    w_in = io["w_in"] if io is not None else p.dram("w_in", [D, INC], F32, "ExternalInput")
    gains = io["gains"] if io is not None else p.dram("gains", [128, 2], F32, "ExternalInput")
    consts = io["consts"] if io is not None else p.dram("consts", [128, 258], F32, "ExternalInput")
    qT = io["qT"] if io is not None else p.dram("qT", [D, TQ], BF16, "ExternalOutput")
    kT = io["kT"] if io is not None else p.dram("kT", [D, TQ], BF16, "ExternalOutput")
    vv = io["v"] if io is not None else p.dram("v", [TQ, D], BF16, "ExternalOutput")
    qiT = io["qiT"] if io is not None else p.dram("qiT", [512, TQ], BF16, "ExternalOutput")
    kiT = io["kiT"] if io is not None else p.dram("kiT", [64, TQ], BF16, "ExternalOutput")
    wi = io["wi"] if io is not None else p.dram("wi", [TQ, 8], F32, "ExternalOutput")

    cm = Common(p)
    ps_ss = p.psum("ps_ss", [128, 512])
    psm = [p.psum(f"psm{i}", [128, 512]) for i in range(2)]
    ps2 = p.psum("ps2", [128, 512])
    ps3 = p.psum("ps3", [128, 512])
    psv = [p.psum(f"psv{i}", [128, 512]) for i in range(2)]

    mod = cm.emit_mod(cT, adaw, adab, 16, ps_ss)
    gm = cm.emit_gm(mod, 8, gnorm)
    cm.alloc_norm()
    cst = p.sbuf("cst", [128, 258], F32)
    p.dma("sp", cst[:], consts[:], consts, cst)
    Rm = p.sbuf("Rm", [128, 128], BF16)
    BD = p.sbuf("BD", [128, 128], BF16)
    p.op("dve", lambda e: e.tensor_copy(Rm[:], cst[:, 2:130]), reads=[cst], writes=[Rm])
    p.op("dve", lambda e: e.tensor_copy(BD[:], cst[:, 130:258]), reads=[cst], writes=[BD])
    gn = p.sbuf("gains_sb", [128, 2], F32)
    p.dma("sp", gn[:], gains[:], gains, gn)
    p.op("dve", lambda e: e.tensor_scalar(gn[:, 0:1], gn[:, 0:1], 0.125, None, ALU.mult), reads=[gn], writes=[gn])
    win = p.sbuf("win", [128, 8, INC], BF16)
    for k in range(8):
        cm.load_cast(win, lambda c0, c1, k=k: win[:, k, c0:c1], w_in, lambda c0, c1, k=k: w_in.t[k * 128:(k + 1) * 128, c0:c1], INC)

    xt = p.sbuf("xt", [128, 8, NT], F32)
    hT = p.sbuf("hT", [128, 8, NT], BF16)
    posi = p.sbuf("posi", [128, NT], I32)
    posf = p.sbuf("posf", [128, NT], F32)
    cosT = p.sbuf("cosT", [128, NT], F32)
    sinT = p.sbuf("sinT", [128, NT], F32)
    Ti = p.sbuf("Ti", [128, NT], I32)
    Tf = p.sbuf("Tf", [128, NT], F32)
    sqb = p.sbuf("sqb", [128, NT], BF16)
    rs = p.sbuf("rs", [128, NT], F32)
    qn = [p.sbuf(f"qn{i}", [128, NT], F32) for i in range(2)]
    qnb = [p.sbuf(f"qnb{i}", [128, NT], BF16) for i in range(2)]
    t2 = [p.sbuf(f"t2_{i}", [128, NT], F32) for i in range(2)]
    ob = [p.sbuf(f"ob{i}", [128, 8, NT], BF16) for i in range(2)]
    obi = p.sbuf("obi", [128, 4, NT], BF16)
    obk = p.sbuf("obk", [64, NT], BF16)
    vb = [p.sbuf(f"vb{i}", [128, 1024], BF16) for i in range(2)]
    wis = p.sbuf("wis", [128, 4, 8], F32)
    epsc = cm.epscol

    xTv = xT.t.rearrange("(k p) t -> p k t", p=128)
    qTv = qT.t.rearrange("(k p) t -> p k t", p=128)
    kTv = kT.t.rearrange("(k p) t -> p k t", p=128)
    qiTv = qiT.t.rearrange("(k p) t -> p k t", p=128)

    def rope_out(src_f32, src_bf, out_ap, nrows, out_buf, i):
        p.op("pe", lambda e: e.matmul(ps3[0:nrows, :], Rm[0:nrows, 0:nrows], src_bf[0:nrows, :], start=True, stop=True), reads=[Rm, src_bf], writes=[ps3])
        p.op("dve", lambda e: e.tensor_tensor(t2[i][0:nrows, :], ps3[0:nrows, :], sinT[0:nrows, :], ALU.mult), reads=[ps3, sinT], writes=[t2[i]])
        p.op("pool", lambda e: e.tensor_tensor(src_f32[0:nrows, :], src_f32[0:nrows, :], cosT[0:nrows, :], ALU.mult), reads=[src_f32, cosT], writes=[src_f32])
        p.op("pool", lambda e: e.tensor_tensor(out_ap, src_f32[0:nrows, :], t2[i][0:nrows, :], ALU.add), reads=[src_f32, t2[i]], writes=[out_buf])

    for t in range(TQ // NT):
        sl = slice(t * NT, (t + 1) * NT)
        p.dma("sp", xt[:], xTv[:, :, sl], xT, xt)
        p.dma("pool", posi[:], pos.t[0:1, sl].to_broadcast([128, NT]), pos, posi)
        p.op("dve", lambda e: e.tensor_copy(posf[:], posi[:]), reads=[posi], writes=[posf])
        p.op("dve", lambda e: e.tensor_scalar(sinT[:], posf[:], cst[:, 0:1], None, ALU.mult), reads=[posf, cst], writes=[sinT])
        p.op("dve", lambda e: e.tensor_scalar(cosT[:], sinT[:], PI / 2, None, ALU.add), reads=[sinT], writes=[cosT])
        emit_range_reduce(p, "dve", sinT, Ti, Tf, NT)
        emit_range_reduce(p, "dve", cosT, Ti, Tf, NT)
        p.op("act", lambda e: e.activation(out=sinT[:], in_=sinT[:], func=AF.Sin), reads=[sinT], writes=[sinT])
        p.op("act", lambda e: e.activation(out=cosT[:], in_=cosT[:], func=AF.Sin), reads=[cosT], writes=[cosT])
        cm.emit_norm(xt, NT, gm, mod, 0, hT, ps_ss)
        for which, (base, gcol, outv, outd) in enumerate(((0, 0, qTv, qT), (D, 1, kTv, kT))):
            for c in range(8):
                i = c % 2
                ps = psm[i]
                for k in range(8):
                    p.op("pe", lambda e, k=k, c=c, ps=ps, base=base: e.matmul(ps[:, :], win[:, k, base + c * 128:base + (c + 1) * 128], hT[:, k, :],
                                                                               start=(k == 0), stop=(k == 7)), reads=[win, hT], writes=[ps])
                p.op("act", lambda e, ps=ps: e.activation(out=sqb[:], in_=ps[:, :], func=AF.Square), reads=[ps], writes=[sqb])
                p.op("pe", lambda e: e.matmul(ps2[:, :], BD[:], sqb[:], start=True, stop=True), reads=[BD, sqb], writes=[ps2])
                p.op("act", lambda e: e.activation(out=rs[:], in_=ps2[:, :], func=AF.Sqrt, bias=epsc[:], scale=1.0 / 64), reads=[ps2, epsc], writes=[rs])
                p.op("dve", lambda e: e.reciprocal(rs[:], rs[:]), reads=[rs], writes=[rs])
                p.op("dve", lambda e, ps=ps, i=i, gcol=gcol: e.scalar_tensor_tensor(qn[i][:], ps[:, :], gn[:, gcol:gcol + 1], rs[:], ALU.mult, ALU.mult),
                     reads=[ps, gn, rs], writes=[qn[i]])
                p.op("act", lambda e, i=i: e.copy(qnb[i][:], qn[i][:]), reads=[qn[i]], writes=[qnb[i]])
                rope_out(qn[i], qnb[i], ob[which][:, c, :], 128, ob[which], i)
            p.dma("sp", outv[:, :, sl], ob[which][:], ob[which], outd)
        for c in range(4):
            i = c % 2
            ps = psm[i]
            for k in range(8):
                p.op("pe", lambda e, k=k, c=c, ps=ps: e.matmul(ps[:, :], win[:, k, 3 * D + c * 128:3 * D + (c + 1) * 128], hT[:, k, :],
                                                               start=(k == 0), stop=(k == 7)), reads=[win, hT], writes=[ps])
            p.op("act", lambda e, ps=ps, i=i: e.activation(out=qn[i][:], in_=ps[:, :], func=AF.Copy, scale=0.125), reads=[ps], writes=[qn[i]])
            p.op("act", lambda e, i=i: e.copy(qnb[i][:], qn[i][:]), reads=[qn[i]], writes=[qnb[i]])
            rope_out(qn[i], qnb[i], obi[:, c, :], 128, obi, i)
        p.dma("sp", qiTv[:, :, sl], obi[:], obi, qiT)
        ps = psm[0]
        for k in range(8):
            p.op("pe", lambda e, k=k, ps=ps: e.matmul(ps[0:64, :], win[:, k, 3 * D + 512:3 * D + 576], hT[:, k, :], start=(k == 0), stop=(k == 7)),
                 reads=[win, hT], writes=[ps])
        p.op("act", lambda e, ps=ps: e.copy(qn[0][0:64, :], ps[0:64, :]), reads=[ps], writes=[qn[0]])
        p.op("act", lambda e: e.copy(qnb[0][0:64, :], qn[0][0:64, :]), reads=[qn[0]], writes=[qnb[0]])
        rope_out(qn[0], qnb[0], obk[0:64, :], 64, obk, 0)
        p.dma("sp", kiT.t[:, sl], obk[:], obk, kiT)
        for s in range(4):
            ssl = slice(s * 128, (s + 1) * 128)
            for k in range(8):
                p.op("pe", lambda e, k=k, ssl=ssl: e.matmul(ps2[:, 0:8], hT[:, k, ssl], win[:, k, 3 * D + 576:3 * D + 584], start=(k == 0), stop=(k == 7)),
                     reads=[hT, win], writes=[ps2])
            p.op("dve", lambda e, s=s: e.tensor_scalar(wis[:, s, :], ps2[:, 0:8], float(8 ** -0.5), None, ALU.mult), reads=[ps2], writes=[wis])
            vbuf = vb[s % 2]
            for half in range(2):
                pv = psv[half]
                for k in range(8):
                    p.op("pe", lambda e, k=k, ssl=ssl, half=half, pv=pv: e.matmul(pv[:, :], hT[:, k, ssl], win[:, k, 2 * D + half * 512:2 * D + (half + 1) * 512],
                                                                                   start=(k == 0), stop=(k == 7)), reads=[hT, win], writes=[pv])
                p.op("act", lambda e, half=half, pv=pv, vbuf=vbuf: e.copy(vbuf[:, half * 512:(half + 1) * 512], pv[:, :]), reads=[pv], writes=[vbuf])
            p.dma("sp", vv.t[t * NT + s * 128:t * NT + (s + 1) * 128, :], vbuf[:], vbuf, vv)
        p.dma("sp", wi.t[sl, :].rearrange("(s p) h -> p s h", p=128), wis[:], wis, wi)
    print("attn_p ops", p.count())
    if standalone:
        return p.finish()
    p.end_phase()


S_ = 16384
NQG = 8
TOPK = 256
NBIS = 22


def EXT(m):
    return min(S_, 2048 * (m + 1))


def attn_a_consts():
    c = np.zeros((128, 512 + 32 + 128), np.float32)
    c[:, 0:512] = (np.arange(512) // 64)[None]
    c[:, 512:544] = (8 * np.arange(32))[None]
    c[:, 544:672] = np.eye(128, dtype=np.float32)
    return c


def build_attn_a(p=None, io=None):
    standalone = p is None
    if standalone:
        nc = bass.Bass("TRN2", target_bir_lowering=False)
        p = Prog(nc)
    p.begin_phase()
    tq = NQG * 512
    qT = io["qT"] if io is not None else p.dram("qT", [D, tq], BF16, "ExternalInput")
    qiT = io["qiT"] if io is not None else p.dram("qiT", [512, tq], BF16, "ExternalInput")
    wi = io["wi"] if io is not None else p.dram("wi", [tq, 8], F32, "ExternalInput")
    qch = io["qch"] if io is not None else p.dram("qch", [128, tq // 128], F32, "ExternalInput")
    KT = io["KT"] if io is not None else p.dram("KT", [D, S_], BF16, "ExternalInput")
    V = io["V"] if io is not None else p.dram("V", [S_, D], BF16, "ExternalInput")
    kiT = io["kiT"] if io is not None else p.dram("kiT", [64, S_], BF16, "ExternalInput")
    xT = io["xT"] if io is not None else p.dram("xT", [D, tq], F32, "ExternalInput")
    w_out = io["w_out"] if io is not None else p.dram("w_out", [D, D], F32, "ExternalInput")
    cT = io["cT"] if io is not None else p.dram("cT", [128, 8], F32, "ExternalInput")
    adaw = io["adaw"] if io is not None else p.dram("adaw", [D, D], F32, "ExternalInput")
    adab = io["adab"] if io is not None else p.dram("adab", [128, 8], F32, "ExternalInput")
    consts = io["consts"] if io is not None else p.dram("consts", [128, 672], F32, "ExternalInput")
    yT = io["yT"] if io is not None else p.dram("yT", [D, tq], F32, "ExternalOutput")
    MT = [io[f"MT{i}"] for i in range(2)] if io is not None else [p.dram(f"MT{i}", [S_, 512], BF16, "Internal") for i in range(2)]

    cm = Common(p, nstage=1)
    psP = [p.psum(f"psP{i}", [128, 1024]) for i in range(2)]
    psT = p.psum("psT", [128, 8, 128], BF16)
    psO = [p.psum(f"psO{i}", [128, 512]) for i in range(2)]
    psX = p.psum("psX", [128, 512])

    mod = cm.emit_mod(cT, adaw, adab, 8, psX)
    cst = p.sbuf("cst", [128, 672], F32)
    p.dma("sp", cst[:], consts[:], consts, cst)
    ident = p.sbuf("ident", [128, 128], BF16)
    p.op("dve", lambda e: e.tensor_copy(ident[:], cst[:, 544:672]), reads=[cst], writes=[ident])
    ones_f = p.sbuf("ones_f", [128, 64], F32)
    p.op("pool", lambda e: e.memset(ones_f[:], 1.0), writes=[ones_f])
    qcs = p.sbuf("qcs", [128, tq // 128], F32)
    p.dma("sp", qcs[:], qch[:], qch, qcs)
    ki2 = p.sbuf("ki2", [128, S_], BF16)
    p.dma("sp", ki2[0:64, :], kiT[:], kiT, ki2)
    p.dma("sp", ki2[64:128, :], kiT[:], kiT, ki2)

    score = p.sbuf("score", [128, S_], F32)
    junk = p.sbuf("junk", [128, 4096], BF16)
    rl = [p.sbuf(f"rl{i}", [128, 1024], F32) for i in range(2)]
    MBp = [p.sbuf("MBp0", [128, 2048], BF16)] * 2
    mts = [p.sbuf(f"mts{i}", [128, 8, 128], BF16) for i in range(2)]
    qit = [p.sbuf(f"qit{i}", [128, 4, 128], BF16) for i in range(2)]
    wit = [p.sbuf(f"wit{i}", [128, 8], F32) for i in range(2)]
    qcm = p.sbuf("qcm", [128, 32], F32)
    sm = p.sbuf("sm", [128, 16], F32)
    kt = [p.sbuf(f"kt{i}", [128, 1024], BF16) for i in range(2)]
    vt = [p.sbuf(f"vt{i}", [128, 8, 2, 65], BF16) for i in range(2)]
    mt = [p.sbuf(f"mt{i}", [128, 8, 512], BF16) for i in range(2)]
    pe_ = [p.sbuf(f"pe{i}", [128, 2, 512], BF16) for i in range(2)]
    pm = [p.sbuf(f"pm{i}", [128, 2, 512], BF16) for i in range(2)]
    qg = p.sbuf("qg", [128, 8, 512], BF16)
    attnT = p.sbuf("attnT", [64, 16, 512], BF16)
    rcs = p.sbuf("rcs", [128, 512], F32)
    bcs = p.sbuf("bcs", [64, 512], F32)
    wof = p.sbuf("wof", [64, 16, 128], F32)
    won = p.sbuf("won", [64, 16, 128], BF16)
    xn = p.sbuf("xn", [128, 512], F32)
    for b_ in vt:
        p.op("pool", lambda e, b_=b_: e.memset(b_[:], 1.0), writes=[b_])

    qiTv = qiT.t.rearrange("(k p) t -> p k t", p=128)
    qTv = qT.t.rearrange("(k p) t -> p k t", p=128)
    iota8 = lambda: cst[:, 0:512]
    LO, HI, MID, TOT, GE, DD, CNT = 0, 1, 2, 3, 4, 5, 6
    col = lambda i: sm[:, i:i + 1]

    for m in range(NQG):
        E = EXT(m)
        nblk = E // 512
        MTm = MT[m % 2]
        for s in range(4):
            qt = 4 * m + s
            qi_ = qit[qt % 2]
            wi_ = wit[qt % 2]
            p.dma("sp", qi_[:], qiTv[:, :, qt * 128:(qt + 1) * 128], qiT, qi_)
            p.dma("sp", wi_[:], wi.t[qt * 128:(qt + 1) * 128, :], wi, wi_)
            p.op("dve", lambda e, qt=qt: e.tensor_scalar(qcm[:], cst[:, 512:544], -1.0, qcs[:, qt:qt + 1], ALU.mult, ALU.add), reads=[cst, qcs], writes=[qcm])
            for kb in range(nblk):
                ksl = slice(kb * 512, (kb + 1) * 512)
                p.op("dve", lambda e, kb=kb, ksl=ksl: e.tensor_scalar(score[:, ksl], iota8(), qcm[:, kb:kb + 1], NEG, ALU.is_gt, ALU.mult),
                     reads=[cst, qcm], writes=[score])
                for hp in range(4):
                    ps = psP[hp % 2]
                    r_ = rl[hp % 2]
                    for hh in range(2):
                        p.op("pe", lambda e, hp=hp, hh=hh, ps=ps, ksl=ksl, qi_=qi_: e.matmul(ps[:, hh * 512:(hh + 1) * 512], qi_[hh * 64:(hh + 1) * 64, hp, :],
                                                                                             ki2[hh * 64:(hh + 1) * 64, ksl], start=True, stop=True),
                             reads=[qi_, ki2], writes=[ps])
                    p.op("act", lambda e, ps=ps, r_=r_: e.activation(out=r_[:], in_=ps[:, :], func=AF.Relu), reads=[ps], writes=[r_])
                    for hh in range(2):
                        h = 2 * hp + hh
                        p.op("dve", lambda e, hh=hh, h=h, r_=r_, ksl=ksl, wi_=wi_: e.scalar_tensor_tensor(score[:, ksl], r_[:, hh * 512:(hh + 1) * 512], wi_[:, h:h + 1],
                                                                                                          score[:, ksl], ALU.mult, ALU.add),
                             reads=[r_, wi_, score], writes=[score])
            bnds = [(c0, min(c0 + 4096, E)) for c0 in range(0, E, 4096)]
            nch = len(bnds)
            p.op("dve", lambda e, E=E: e.tensor_reduce(out=col(HI), in_=score[:, 0:E], axis=AX.X, op=ALU.max), reads=[score], writes=[sm])
            p.op("dve", lambda e: e.tensor_scalar(col(HI), col(HI), 1.0, None, ALU.add), reads=[sm], writes=[sm])
            for c_, (b0, b1) in enumerate(bnds):
                p.op("dve", lambda e, c_=c_, b0=b0, b1=b1: e.tensor_scalar(junk[:, 0:b1 - b0], score[:, b0:b1], -1.0e4, None, ALU.max, ALU.min,
                                                                          accum_out=sm[:, CNT + c_:CNT + c_ + 1]), reads=[score], writes=[junk, sm])
            p.op("dve", lambda e, nch=nch: e.tensor_reduce(out=col(LO), in_=sm[:, CNT:CNT + nch], axis=AX.X, op=ALU.min), reads=[sm], writes=[sm])
            for it in range(NBIS):
                p.op("dve", lambda e: e.tensor_tensor(col(MID), col(LO), col(HI), ALU.add), reads=[sm], writes=[sm])
                p.op("dve", lambda e: e.tensor_scalar(col(MID), col(MID), 0.5, None, ALU.mult), reads=[sm], writes=[sm])
                for c_, (b0, b1) in enumerate(bnds):
                    p.op("dve", lambda e, c_=c_, b0=b0, b1=b1: e.tensor_scalar(junk[:, 0:b1 - b0], score[:, b0:b1], col(MID), None, ALU.is_ge, ALU.add,
                                                                              accum_out=sm[:, CNT + c_:CNT + c_ + 1]), reads=[score, sm], writes=[junk, sm])
                p.op("dve", lambda e, nch=nch: e.tensor_reduce(out=col(TOT), in_=sm[:, CNT:CNT + nch], axis=AX.X, op=ALU.add), reads=[sm], writes=[sm])
                p.op("dve", lambda e: e.tensor_scalar(col(GE), col(TOT), TOPK - 0.5, None, ALU.is_gt), reads=[sm], writes=[sm])
                p.op("dve", lambda e: e.tensor_tensor(col(DD), col(MID), col(LO), ALU.subtract), reads=[sm], writes=[sm])
                p.op("dve", lambda e: e.scalar_tensor_tensor(col(LO), col(DD), col(GE), col(LO), ALU.mult, ALU.add), reads=[sm], writes=[sm])
                p.op("dve", lambda e: e.tensor_tensor(col(DD), col(HI), col(MID), ALU.subtract), reads=[sm], writes=[sm])
                p.op("dve", lambda e: e.scalar_tensor_tensor(col(HI), col(DD), col(GE), col(MID), ALU.mult, ALU.add), reads=[sm], writes=[sm])
            for pc in range(E // 2048):
                mb = MBp[pc % 2]
                p.op("dve", lambda e, pc=pc, mb=mb: e.tensor_scalar(mb[:], score[:, pc * 2048:(pc + 1) * 2048], col(LO), None, ALU.is_ge), reads=[score, sm], writes=[mb])
                for half in range(2):
                    mt_s = mts[half]
                    for i in range(8):
                        ii = half * 8 + i
                        p.op("pe", lambda e, i=i, ii=ii, mb=mb: e.transpose(psT[:, i, :], mb[:, ii * 128:(ii + 1) * 128], ident[:]), reads=[mb, ident], writes=[psT])
                    p.op("act", lambda e, mt_s=mt_s: e.copy(mt_s[:], psT[:]), reads=[psT], writes=[mt_s])
                    k0 = pc * 2048 + half * 1024
                    p.dma("sp", MTm.t[k0:k0 + 1024, s * 128:(s + 1) * 128].rearrange("(i p) q -> p i q", p=128), mt_s[:], mt_s, MTm)
        p.dma("sp", qg[:], qTv[:, :, m * 512:(m + 1) * 512], qT, qg)
        npc = E // 1024
        for c in range(8):
            for pc in range(npc):
                kt_, vt_, mt_ = kt[pc % 2], vt[pc % 2], mt[pc % 2]
                p.dma("sp", kt_[:], KT.t[c * 128:(c + 1) * 128, pc * 1024:(pc + 1) * 1024], KT, kt_)
                for hh in range(2):
                    p.dma("pool", vt_[:, :, hh, 0:64], V.t[pc * 1024:(pc + 1) * 1024, (2 * c + hh) * 64:(2 * c + hh + 1) * 64].rearrange("(i p) d -> p i d", p=128), V, vt_)
                p.dma("sp", mt_[:], MTm.t[pc * 1024:(pc + 1) * 1024, :].rearrange("(i p) q -> p i q", p=128), MTm, mt_)
                for i in range(8):
                    kti = pc * 8 + i
                    ps = psP[kti % 2]
                    pe_i, pm_i = pe_[kti % 2], pm[kti % 2]
                    for hh in range(2):
                        p.op("pe", lambda e, hh=hh, i=i, ps=ps, kt_=kt_, c=c: e.matmul(ps[:, hh * 512:(hh + 1) * 512], kt_[hh * 64:(hh + 1) * 64, i * 128:(i + 1) * 128],
                                                                                       qg[hh * 64:(hh + 1) * 64, c, :], start=True, stop=True),
                             reads=[kt_, qg], writes=[ps])
                    p.op("act", lambda e, ps=ps, pe_i=pe_i: e.activation(out=pe_i[:].rearrange("p a b -> p (a b)"), in_=ps[:, :], func=AF.Exp), reads=[ps], writes=[pe_i])
                    eng = "dve" if kti % 2 == 0 else "pool"
                    p.op(eng, lambda e, i=i, pe_i=pe_i, pm_i=pm_i, mt_=mt_: e.tensor_tensor(pm_i[:], pe_i[:], mt_[:, i:i + 1, :].to_broadcast([128, 2, 512]), ALU.mult),
                         reads=[pe_i, mt_], writes=[pm_i])
                    for hh in range(2):
                        p.op("pe", lambda e, hh=hh, i=i, vt_=vt_, pm_i=pm_i, kti=kti: e.matmul(psO[hh][0:65, :], vt_[:, i, hh, :], pm_i[:, hh, :],
                                                                                               start=(kti == 0), stop=(kti == npc * 8 - 1)),
                             reads=[vt_, pm_i], writes=[psO[hh]])
            for hh in range(2):
                h = 2 * c + hh
                p.op("dve", lambda e, hh=hh: e.reciprocal(rcs[64:65, :], psO[hh][64:65, :]), reads=[psO[hh]], writes=[rcs])
                p.op("pe", lambda e: e.matmul(psX[0:64, :], ones_f[64:65, 0:64], rcs[64:65, :], start=True, stop=True), reads=[ones_f, rcs], writes=[psX])
                p.op("act", lambda e: e.copy(bcs[:], psX[0:64, :]), reads=[psX], writes=[bcs])
                p.op("dve", lambda e, hh=hh, h=h: e.tensor_tensor(attnT[:, h, :], psO[hh][0:64, :], bcs[:], ALU.mult), reads=[psO[hh], bcs], writes=[attnT])
        for n in range(8):
            p.dma("sp", wof[:], w_out.t[:, n * 128:(n + 1) * 128].rearrange("(h d) c -> d h c", d=64), w_out, wof)
            p.op("pool", lambda e: e.tensor_copy(won[:], wof[:]), reads=[wof], writes=[won])
            p.dma("pool", xn[:], xT.t[n * 128:(n + 1) * 128, m * 512:(m + 1) * 512], xT, xn)
            for h in range(16):
                p.op("pe", lambda e, h=h: e.matmul(psX[:, :], won[:, h, :], attnT[:, h, :], start=(h == 0), stop=(h == 15)), reads=[won, attnT], writes=[psX])
            p.op("dve", lambda e, n=n: e.scalar_tensor_tensor(xn[:], psX[:, :], mod[:, n:n + 1], xn[:], ALU.mult, ALU.add), reads=[psX, mod, xn], writes=[xn])
            p.dma("sp", yT.t[n * 128:(n + 1) * 128, m * 512:(m + 1) * 512], xn[:], xn, yT)
    print("attn_a ops", p.count())
    if standalone:
        return p.finish()
    p.end_phase()


def _setg(**kw):
    globals().update(kw)


def build_fused(seq):
    nc = bass.Bass("TRN2", target_bir_lowering=False)
    p = Prog(nc)
    I = lambda name, shape, dt=F32: p.dram(name, shape, dt, "ExternalInput")
    Sx = lambda name, shape, dt=F32: p.dram(name, shape, dt, "Internal")
    x0T = I("x0T", [D, seq]); pos = I("pos", [1, seq], I32); cT = I("cT", [128, 8])
    ada_w = I("ada_w", [4, D, 6 * D]); adab = I("adab", [4, 128, 48])
    nmix = I("nmix", [4, 128, 8]); nffn = I("nffn", [4, 128, 8])
    w_in = I("w_in", [2, D, INC]); gains = I("gains", [2, 128, 2]); w_out = I("w_out", [2, D, D])
    pconsts = I("pconsts", [128, 258]); aconsts = I("aconsts", [128, 672]); qch = I("qch", [128, seq // 128])
    s_adaw = I("s_adaw", [2, 4, D, 512]); s_adab = I("s_adab", [2, 4, 128, 4]); s_gn = I("s_gn", [2, 4, 128, 2])
    prmR = I("prmR", [2, 4, 128, 5, 128]); prmS = I("prmS", [2, 4, 128, 3, 16]); cS = I("cS", [2, 4, 128, 16, 16])
    dsk = I("dsk", [2, 4, 128, 2]); sconsts = I("sconsts", [128, 10 + CT])
    w_glu = I("w_glu", [2, D, 2 * D])
    w_up = I("w_up", [4, D, FF2]); w_down = I("w_down", [4, FF, D]); conv_w = I("conv_w", [4, 128, 44, 3]); conv_b = I("conv_b", [4, 128, 44])
    hflag = I("hflag", [128, 1])
    outT = p.dram("outT", [D, seq], F32, "ExternalOutput")
    XM = Sx("XM", [D, 2 + seq]); XO = Sx("XO", [D, seq]); G = Sx("G", [D, seq])
    QT = Sx("QT", [D, seq], BF16); KTs = Sx("KTs", [D, seq], BF16); VV = Sx("VV", [seq, D], BF16)
    QIT = Sx("QIT", [512, seq], BF16); KIT = Sx("KIT", [64, seq], BF16); WI = Sx("WI", [seq, 8])
    MT0 = Sx("MT0", [seq, 512], BF16); MT1 = Sx("MT1", [seq, 512], BF16)

    p.begin_phase()
    z = p.sbuf("zeros", [128, 8, 2], F32)
    p.op("pool", lambda e: e.memset(z[:], 0.0), writes=[z])
    p.dma("sp", XM.t[:, 0:2].rearrange("(k p) t -> p k t", p=128), z[:], z, XM)
    p.end_phase()

    xin = x0T
    for L in range(4):
        J = L // 2
        gm_aw = ada_w.view(ada_w.t[L][:, 2 * D:3 * D])
        gm_ab = adab.view(adab.t[L][:, 16:24])
        xmid = XM.view(XM.t[:, 2:2 + seq])
        if L % 2 == 0:
            _setg(TQ=seq)
            build_attn_p(p, {"xT": xin, "pos": pos, "cT": cT, "adaw": ada_w.view(ada_w.t[L][:, 0:2 * D]), "adab": adab.view(adab.t[L][:, 0:16]),
                             "gnorm": nmix.view(nmix.t[L]), "w_in": w_in.view(w_in.t[J]), "gains": gains.view(gains.t[J]), "consts": pconsts,
                             "qT": QT, "kT": KTs, "v": VV, "qiT": QIT, "kiT": KIT, "wi": WI})
            _setg(S_=seq, NQG=seq // 512, EXT=lambda m, seq=seq: min(seq, 2048 * ((512 * (m + 1) + 2047) // 2048)))
            build_attn_a(p, {"qT": QT, "qiT": QIT, "wi": WI, "qch": qch, "KT": KTs, "V": VV, "kiT": KIT, "xT": xin,
                             "w_out": w_out.view(w_out.t[J]), "cT": cT, "adaw": gm_aw, "adab": gm_ab, "consts": aconsts,
                             "yT": xmid, "MT0": MT0, "MT1": MT1})
        else:
            for j in range(4):
                _setg(SEQ=seq, OWN0=2 * j)
                build_scan(p, {"xT": xin, "cT": cT, "adaw": s_adaw.view(s_adaw.t[J][j]), "adab": s_adab.view(s_adab.t[J][j]),
                               "gnorm": s_gn.view(s_gn.t[J][j]), "prmR": prmR.view(prmR.t[J][j]), "prmS": prmS.view(prmS.t[J][j]),
                               "cS": cS.view(cS.t[J][j]), "dsk": dsk.view(dsk.t[J][j]), "consts": sconsts,
                               "yT": G.view(G.t[256 * j:256 * j + 256, :])})
            _setg(TC=seq)
            build_glu(p, {"xT": xin, "gT": G, "cT": cT, "adaw": gm_aw, "adab": gm_ab, "w_glu": w_glu.view(w_glu.t[J]), "yT": xmid})
        _setg(TC=seq, NTILES=seq // NT)
        build_ffn(p, {"xT": XM, "cT": cT, "adaw": ada_w.view(ada_w.t[L][:, 3 * D:6 * D]), "adab": adab.view(adab.t[L][:, 24:48]),
                      "gnorm": nffn.view(nffn.t[L]), "w_up": w_up.view(w_up.t[L]), "w_down": w_down.view(w_down.t[L]),
                      "conv_w": conv_w.view(conv_w.t[L]), "conv_b": conv_b.view(conv_b.t[L]), "hflag": hflag,
                      "yT": outT if L == 3 else XO})
        xin = XO
    return p.finish()


def _scan_consts():
    consts = np.zeros((128, 10 + CT), np.float32)
    consts[:64, 0] = 1
    consts[64:, 0] = -1
    for gl in range(8):
        consts[16 * gl:16 * gl + 16, 1 + gl] = 1
    consts[:, 10:] = np.arange(1, CT + 1, dtype=np.float32)[None]
    return consts


def _scan_params(inp, J, j):
    gsl = slice(16 * j, 16 * j + 16)
    are, aim, ldt = inp["ssm_a_re"][J][gsl], inp["ssm_a_im"][J][gsl], inp["ssm_log_dt"][J][gsl]
    bre, bim = inp["ssm_b_re"][J][gsl], inp["ssm_b_im"][J][gsl]
    cre, cim = inp["ssm_c_re"][J][gsl], inp["ssm_c_im"][J][gsl]

    def layR_gp(a):
        a4 = a.reshape(2, 8, 64)
        return np.broadcast_to(a4.transpose(1, 0, 2)[:, None, :, :], (8, 16, 2, 64)).reshape(128, 128)

    def layR_b(bb):
        return bb.reshape(2, 8, 64, 16).transpose(1, 3, 0, 2).reshape(128, 128)

    def layS(a):
        return np.concatenate([a.T, a.T], 0)

    ldt2 = np.broadcast_to(ldt[:, None], (16, 64))
    prmR = np.stack([layR_gp(are), layR_gp(aim), layR_gp(ldt2), layR_b(bre), layR_b(bim)], 1)
    prmS = np.stack([layS(are), layS(aim), layS(ldt2)], 1)
    cS = np.concatenate([cre.transpose(2, 0, 1), cim.transpose(2, 0, 1)], 0)
    return prmR.astype(np.float32), prmS.astype(np.float32), cS.astype(np.float32)


def fused_inputs(inp, b, seq):
    f32 = np.float32
    ada_w, ada_b = inp["ada_w"], inp["ada_b"]
    s_adaw = np.zeros((2, 4, D, 512), f32); s_adab = np.zeros((2, 4, 128, 4), f32); s_gn = np.zeros((2, 4, 128, 2), f32)
    prmR = np.zeros((2, 4, 128, 5, 128), f32); prmS = np.zeros((2, 4, 128, 3, 16), f32); cS = np.zeros((2, 4, 128, 16, 16), f32)
    dsk = np.zeros((2, 4, 128, 2), f32)
    for J in range(2):
        L = 2 * J + 1
        for j in range(4):
            cols = np.r_[256 * j:256 * j + 256, D + 256 * j:D + 256 * j + 256]
            s_adaw[J, j] = ada_w[L][:, cols]
            s_adab[J, j] = col_layout(ada_b[L][cols], 4)
            s_gn[J, j] = col_layout(inp["norm_mix"][L][256 * j:256 * j + 256], 2)
            prmR[J, j], prmS[J, j], cS[J, j] = _scan_params(inp, J, j)
            dsk[J, j] = col_layout(inp["ssm_d"][J][256 * j:256 * j + 256], 2)
    idx = np.arange(seq)
    return {
        "x0T": np.ascontiguousarray(inp["x"][b, :seq].T, dtype=f32),
        "pos": np.ascontiguousarray(inp["positions"][b, :seq][None].astype(np.int32)),
        "cT": col_layout(inp["c"][b], 8),
        "ada_w": np.ascontiguousarray(ada_w, dtype=f32),
        "adab": np.stack([col_layout(ada_b[L], 48) for L in range(4)]),
        "nmix": np.stack([col_layout(inp["norm_mix"][L], 8) for L in range(4)]),
        "nffn": np.stack([col_layout(inp["norm_ffn"][L], 8) for L in range(4)]),
        "w_in": np.ascontiguousarray(inp["attn_w_in"], dtype=f32),
        "gains": np.stack([np.stack([np.tile(inp["attn_q_gain"][J], 2), np.tile(inp["attn_k_gain"][J], 2)], 1) for J in range(2)]).astype(f32),
        "w_out": np.ascontiguousarray(inp["attn_w_out"], dtype=f32),
        "pconsts": attn_consts(), "aconsts": attn_a_consts(),
        "qch": np.ascontiguousarray((idx.reshape(-1, 128).T // 64).astype(f32)),
        "s_adaw": s_adaw, "s_adab": s_adab, "s_gn": s_gn, "prmR": prmR, "prmS": prmS, "cS": cS, "dsk": dsk,
        "sconsts": _scan_consts(),
        "w_glu": np.ascontiguousarray(inp["ssm_w_glu"], dtype=f32),
        "w_up": np.ascontiguousarray(inp["ffn_w_up"], dtype=f32), "w_down": np.ascontiguousarray(inp["ffn_w_down"], dtype=f32),
        "conv_w": np.stack([np.ascontiguousarray(inp["ffn_conv_w"][L].T.reshape(44, 128, 3).transpose(1, 0, 2)) for L in range(4)]).astype(f32),
        "conv_b": np.stack([col_layout(inp["ffn_conv_b"][L], 44) for L in range(4)]).astype(f32),
        "hflag": np.zeros((128, 1), f32),
    }


_NC = {}


def run_fused(inp, seq):
    if seq not in _NC:
        _NC[seq] = build_fused(seq)
    in_maps = [fused_inputs(inp, b, seq) for b in range(2)]
    res = run_bass_kernel_spmd(_NC[seq], in_maps, core_ids=[0, 1])
    return np.stack([np.ascontiguousarray(res.results[b]["outT"].T) for b in range(2)])


def kernel(**inputs):
    inp = {k: np.asarray(v) for k, v in inputs.items()}
    return run_fused(inp, 16384).astype(np.float32)
```
